# Optimizing a Trainium2 kernel written in Bass

```python
import jax, jax.numpy as jnp
from jax import lax
import numpy as np

D_MODEL = 4096
BATCH = 2
SEQ = 4096
DEPTH = 2
DEC_BATCH = 2
DEC_SEQ = 8192
PAST_LEN = 128

D_FF = 11008
GLA_HEADS = 4
GLA_DK = D_MODEL // 16
GLA_DV = D_MODEL // 8
GLA_RANK = 16
GLA_TAU = 16.0
GLA_CHUNK = 64
ATTN_KV_HEADS = 8
ATTN_HEAD_DIM = 128
DIL_WINDOWS = (128, 512, 2048)
DIL_RATES = (1, 4, 16)
N_DIL_GROUPS = 3
ALIBI_MAX_BIAS = 8.0
NORM_EPS = 1e-6
NEG_INF = -1e30

GLA_QK_W = GLA_HEADS * GLA_DK
GLA_V_W = GLA_HEADS * GLA_DV
ATTN_Q_W = N_DIL_GROUPS * ATTN_KV_HEADS * ATTN_HEAD_DIM
ATTN_KV_W = ATTN_KV_HEADS * ATTN_HEAD_DIM
IN_SIZES = (GLA_QK_W, GLA_QK_W, GLA_V_W, GLA_V_W, GLA_RANK, GLA_RANK,
            ATTN_Q_W, ATTN_KV_W, ATTN_KV_W, 2 * D_MODEL)
IN_COLS = sum(IN_SIZES)

kernel_name = "hybrid_gla_dilated_alibi_macaron_encoder"


def rms_norm(x, g):
    xf = x.astype(jnp.float32)
    y = xf * lax.rsqrt(jnp.mean(xf * xf, axis=-1, keepdims=True) + NORM_EPS)
    return (y * g.astype(jnp.float32)).astype(x.dtype)


def swiglu(x, w_gate, w_up, w_down):
    return (jax.nn.silu(x @ w_gate) * (x @ w_up)) @ w_down


def alibi_slopes(n):
    return jnp.exp2(-ALIBI_MAX_BIAS * jnp.arange(1, n + 1, dtype=jnp.float32) / n)


def gla_direction(q, k, v, log_a, strict):
    B, T, H, DK = q.shape
    DV = v.shape[-1]
    N = T // GLA_CHUNK

    def chunks(x):
        return x.reshape(B, N, GLA_CHUNK, H, x.shape[-1]).transpose(1, 0, 3, 2, 4)

    mask = jnp.tril(jnp.ones((GLA_CHUNK, GLA_CHUNK), jnp.float32), -1 if strict else 0)

    def step(S, inp):
        qi, ki, vi, ai = inp
        b = jnp.cumsum(ai, axis=2)
        b_last = b[:, :, -1:, :]
        q_dec = qi * jnp.exp(b)
        scores = jnp.einsum('bhid,bhjd->bhij', q_dec, ki * jnp.exp(-b)) * mask
        o = (jnp.einsum('bhij,bhjv->bhiv', scores, vi)
             + jnp.einsum('bhid,bhdv->bhiv', q_dec, S))
        S = (jnp.exp(b_last[:, :, 0, :, None]) * S
             + jnp.einsum('bhjd,bhjv->bhdv', ki * jnp.exp(b_last - b), vi))
        return S, o

    S0 = jnp.zeros((B, H, DK, DV), jnp.float32)
    _, o = lax.scan(step, S0, (chunks(q), chunks(k), chunks(v), chunks(log_a)))
    return o.transpose(1, 0, 3, 2, 4).reshape(B, T, H, DV)


def gla_branch(q, k, v, r, lr_f, lr_b, up_f, bias_f, up_b, bias_b, norm_g):
    B, T, _ = q.shape
    f32 = jnp.float32
    qh = q.astype(f32).reshape(B, T, GLA_HEADS, GLA_DK) * (GLA_DK ** -0.5)
    kh = k.astype(f32).reshape(B, T, GLA_HEADS, GLA_DK)
    vh = v.astype(f32).reshape(B, T, GLA_HEADS, GLA_DV)
    la_f = (jax.nn.log_sigmoid((lr_f @ up_f + bias_f).astype(f32)) / GLA_TAU).reshape(B, T, GLA_HEADS, GLA_DK)
    la_b = (jax.nn.log_sigmoid((lr_b @ up_b + bias_b).astype(f32)) / GLA_TAU).reshape(B, T, GLA_HEADS, GLA_DK)
    flip = lambda a: a[:, ::-1]
    o = (gla_direction(qh, kh, vh, la_f, False)
         + flip(gla_direction(flip(qh), flip(kh), flip(vh), flip(la_b), True)))
    o = o * lax.rsqrt(jnp.mean(o * o, axis=-1, keepdims=True) + NORM_EPS)
    o = o.reshape(B, T, GLA_V_W) * norm_g.astype(f32)
    return (o * jax.nn.silu(r.astype(f32))).astype(q.dtype)


def dilated_group_attention(q, k, v, window, dilation, slopes):
    B, T, H, E = q.shape
    f32 = jnp.float32
    R = window // (2 * dilation)
    L = T // dilation
    nb = -(-L // R)
    Lp = nb * R

    def by_residue(x):
        return x.reshape(B, L, dilation, H, E).transpose(0, 2, 1, 3, 4)

    qs = jnp.pad(by_residue(q), ((0, 0), (0, 0), (0, Lp - L), (0, 0), (0, 0))).reshape(B, dilation, nb, R, H, E)

    def windows(x):
        xp = jnp.pad(by_residue(x), ((0, 0), (0, 0), (R, Lp - L + R), (0, 0), (0, 0)))
        xp = xp.reshape(B, dilation, nb + 2, R, H, E)
        return jnp.concatenate([xp[:, :, :-2], xp[:, :, 1:-1], xp[:, :, 2:]], axis=3)

    kw, vw = windows(k), windows(v)
    n_q = jnp.arange(Lp).reshape(nb, R)
    n_k = jnp.arange(nb)[:, None] * R - R + jnp.arange(3 * R)[None, :]
    rel = jnp.abs(n_k[:, None, :] - n_q[:, :, None])
    valid = (rel <= R) & ((n_k >= 0) & (n_k < L))[:, None, :]
    bias = -slopes.astype(f32)[None, :, None, None] * (dilation * rel).astype(f32)[:, None]
    scores = jnp.einsum('bgnqhe,bgnkhe->bgnhqk', qs.astype(f32), kw.astype(f32)) * (E ** -0.5) + bias
    scores = jnp.where(valid[:, None], scores, NEG_INF)
    m = jnp.max(scores, axis=-1, keepdims=True)
    p = jnp.exp(scores - m)
    s = jnp.sum(p, axis=-1, keepdims=True)
    lse = jnp.transpose((m + jnp.log(s))[..., 0], (0, 1, 2, 4, 3))
    o = jnp.einsum('bgnhqk,bgnkhe->bgnqhe', p, vw.astype(f32))
    o = o / jnp.transpose(s, (0, 1, 2, 4, 3, 5))

    def back(x):
        x = x.reshape((B, dilation, Lp) + x.shape[4:])[:, :, :L]
        return jnp.moveaxis(x, 1, 2).reshape((B, T) + x.shape[3:])

    return back(o), back(lse)


def dilated_attention_branch(q, k, v):
    B, T, _ = q.shape
    qh = q.reshape(B, T, N_DIL_GROUPS, ATTN_KV_HEADS, ATTN_HEAD_DIM)
    kh = k.reshape(B, T, ATTN_KV_HEADS, ATTN_HEAD_DIM)
    vh = v.reshape(B, T, ATTN_KV_HEADS, ATTN_HEAD_DIM)
    slopes = alibi_slopes(N_DIL_GROUPS * ATTN_KV_HEADS)
    outs, lses = [], []
    for g in range(N_DIL_GROUPS):
        o_g, lse_g = dilated_group_attention(qh[:, :, g], kh, vh, DIL_WINDOWS[g], DIL_RATES[g],
                                             slopes[g * ATTN_KV_HEADS:(g + 1) * ATTN_KV_HEADS])
        outs.append(o_g)
        lses.append(lse_g)
    wts = jax.nn.softmax(jnp.stack(lses, axis=0), axis=0)
    o = jnp.sum(wts[..., None] * jnp.stack(outs, axis=0), axis=0)
    return o.reshape(B, T, ATTN_KV_W).astype(q.dtype)


def mixer(h, w_in, up_f, bias_f, up_b, bias_b, gla_norm_g, proj_gla, proj_attn, w_out):
    split_points = tuple(int(c) for c in np.cumsum(IN_SIZES)[:-1])
    z = h @ w_in
    gq, gk, gv, gr, lr_f, lr_b, aq, ak, av, gate_logits = jnp.split(z, split_points, axis=-1)
    o_gla = gla_branch(gq, gk, gv, gr, lr_f, lr_b, up_f, bias_f, up_b, bias_b, gla_norm_g)
    o_att = dilated_attention_branch(aq, ak, av)
    g_gla, g_att = jnp.split(jax.nn.sigmoid(gate_logits), 2, axis=-1)
    merged = g_gla * (o_gla @ proj_gla) + g_att * (o_att @ proj_attn)
    return merged @ w_out


def encoder_trunk(x, p):
    (f1_pre, f1_wg, f1_wu, f1_wd, f1_post, mix_pre, w_in, up_f, b_f, up_b, b_b, gla_g,
     proj_gla, proj_attn, w_out, mix_post, f2_pre, f2_wg, f2_wu, f2_wd, f2_post) = p
    for l in range(DEPTH):
        x = x + 0.5 * rms_norm(swiglu(rms_norm(x, f1_pre[l]), f1_wg[l], f1_wu[l], f1_wd[l]), f1_post[l])
        x = x + rms_norm(mixer(rms_norm(x, mix_pre[l]), w_in[l], up_f[l], b_f[l], up_b[l], b_b[l],
                               gla_g[l], proj_gla[l], proj_attn[l], w_out[l]), mix_post[l])
        x = x + 0.5 * rms_norm(swiglu(rms_norm(x, f2_pre[l]), f2_wg[l], f2_wu[l], f2_wd[l]), f2_post[l])
    return x


def setup_inputs(seed: int = 0) -> dict:
    key = jax.random.key(seed)
    ks = jax.random.split(key, 24)
    f32 = jnp.float32
    L, D = DEPTH, D_MODEL

    def w(k, shape, fan_in):
        return jax.random.normal(k, shape, f32) * (fan_in ** -0.5)

    def gain(k, shape):
        return 1.0 + 0.02 * jax.random.normal(k, shape, f32)

    def small(k, shape):
        return 0.1 * jax.random.normal(k, shape, f32)

    return {
        "x_prompt": jax.random.normal(ks[0], (BATCH, SEQ, D), f32),
        "x_sample": jax.random.normal(ks[1], (DEC_BATCH, DEC_SEQ, D), f32),
        "ffn1_pre_g": gain(ks[2], (L, D)),
        "ffn1_w_gate": w(ks[3], (L, D, D_FF), D),
        "ffn1_w_up": w(ks[4], (L, D, D_FF), D),
        "ffn1_w_down": w(ks[5], (L, D_FF, D), D_FF),
        "ffn1_post_g": gain(ks[6], (L, D)),
        "mix_pre_g": gain(ks[7], (L, D)),
        "w_in": w(ks[8], (L, D, IN_COLS), D),
        "gla_decay_up_fwd": w(ks[9], (L, GLA_RANK, GLA_QK_W), GLA_RANK),
        "gla_decay_bias_fwd": small(ks[10], (L, GLA_QK_W)),
        "gla_decay_up_bwd": w(ks[11], (L, GLA_RANK, GLA_QK_W), GLA_RANK),
        "gla_decay_bias_bwd": small(ks[12], (L, GLA_QK_W)),
        "gla_norm_g": gain(ks[13], (L, GLA_V_W)),
        "proj_gla": w(ks[14], (L, GLA_V_W, D), GLA_V_W),
        "proj_attn": w(ks[15], (L, ATTN_KV_W, D), ATTN_KV_W),
        "w_out": w(ks[16], (L, D, D), D),
        "mix_post_g": gain(ks[17], (L, D)),
        "ffn2_pre_g": gain(ks[18], (L, D)),
        "ffn2_w_gate": w(ks[19], (L, D, D_FF), D),
        "ffn2_w_up": w(ks[20], (L, D, D_FF), D),
        "ffn2_w_down": w(ks[21], (L, D_FF, D), D_FF),
        "ffn2_post_g": gain(ks[22], (L, D)),
    }


def reference(x_prompt, x_sample, ffn1_pre_g, ffn1_w_gate, ffn1_w_up, ffn1_w_down, ffn1_post_g,
              mix_pre_g, w_in, gla_decay_up_fwd, gla_decay_bias_fwd, gla_decay_up_bwd,
              gla_decay_bias_bwd, gla_norm_g, proj_gla, proj_attn, w_out, mix_post_g,
              ffn2_pre_g, ffn2_w_gate, ffn2_w_up, ffn2_w_down, ffn2_post_g):
    params = (ffn1_pre_g, ffn1_w_gate, ffn1_w_up, ffn1_w_down, ffn1_post_g,
              mix_pre_g, w_in, gla_decay_up_fwd, gla_decay_bias_fwd, gla_decay_up_bwd,
              gla_decay_bias_bwd, gla_norm_g, proj_gla, proj_attn, w_out, mix_post_g,
              ffn2_pre_g, ffn2_w_gate, ffn2_w_up, ffn2_w_down, ffn2_post_g)
    y_prompt = encoder_trunk(x_prompt, params)
    y_sample = encoder_trunk(x_sample, params)
    return (y_prompt, y_sample)
```

```python
from contextlib import ExitStack
import numpy as np
import concourse.bass as bass
import concourse.mybir as mybir
from concourse.bass_utils import run_bass_kernel_spmd

F32 = mybir.dt.float32
BF16 = mybir.dt.bfloat16
AF = mybir.ActivationFunctionType
ALU = mybir.AluOpType
EPS = 1e-6
TAU = 16.0


class Cfg:
    def __init__(self, D=4096, F=11008, T=8192, L=2, H=8, GH=4, TT=512):
        self.D, self.F, self.T, self.L, self.H, self.GH, self.TT = D, F, T, L, H, GH, TT
        self.E = 128
        self.NG = 3
        self.DK = 256
        self.DV = 512
        self.RANK = 16
        self.KD = D // 128
        self.KF = F // 128
        self.NT = T // TT
        self.QK = GH * self.DK
        self.VW = GH * self.DV
        self.AQ = self.NG * H * self.E
        self.AKV = H * self.E
        o = 0
        self.o_gq = o; o += self.QK
        self.o_gk = o; o += self.QK
        self.o_gv = o; o += self.VW
        self.o_gr = o; o += self.VW
        self.o_lf = o; o += self.RANK
        self.o_lb = o; o += self.RANK
        self.o_aq = o; o += self.AQ
        self.o_ak = o; o += self.AKV
        self.o_av = o; o += self.AKV
        self.o_gate = o; o += 2 * D
        self.INC = o
        self.DIL = (1, 4, 16)
        self.PAD = 64 * 16


class Buf:
    def __init__(self, name, nowaw=False):
        self.name = name
        self.w = {}
        self.r = {}
        self.nowaw = nowaw
        self.dsem = None


class Eng:
    def __init__(self, nc, h, name):
        self.h = h
        self.name = name
        self.sem = nc.alloc_semaphore("e_" + name)
        self.cnt = 0
        self.seen = {}


class KB:
    def __init__(self, nc):
        self.nc = nc
        self.pe = Eng(nc, nc.tensor, "pe")
        self.act = Eng(nc, nc.scalar, "act")
        self.dve = Eng(nc, nc.vector, "dve")
        self.pool = Eng(nc, nc.gpsimd, "pool")
        self.sp = Eng(nc, nc.sync, "sp")
        self.engs = [self.pe, self.act, self.dve, self.pool, self.sp]
        self.dsems = []
        self.semcache = {}

    def _wait(self, eng, sem, val, same_ok=False):
        k = id(sem)
        if (sem is eng.sem and not same_ok) or eng.seen.get(k, 0) >= val:
            return
        eng.h.wait_ge(sem, val)
        eng.seen[k] = val

    def _waits(self, eng, reads, writes):
        raw, oth = {}, {}

        def add(need, d):
            for s, v in d.items():
                if s not in need or need[s][1] < v[1]:
                    need[s] = v

        for b in reads:
            add(raw, b.w)
        for b in writes:
            if not b.nowaw:
                add(oth, b.w)
            add(oth, b.r)
        for s, (sem, val) in raw.items():
            self._wait(eng, sem, val, same_ok=(eng is not self.pe))
        for s, (sem, val) in oth.items():
            self._wait(eng, sem, val)

    def _record(self, tok, reads, writes):
        key = id(tok[0])
        for b in reads:
            b.r[key] = tok
        for b in writes:
            if b.nowaw:
                b.w[key] = tok
            else:
                b.w = {key: tok}
                b.r = {}

    def op(self, eng, fn, r=(), w=()):
        self._waits(eng, r, w)
        ins = fn()
        eng.cnt += 1
        ins.then_inc(eng.sem, 1)
        self._record((eng.sem, eng.cnt), r, w)
        return ins

    def group(self, eng, fns, r=(), w=()):
        self._waits(eng, r, w)
        ins = None
        for fn in fns:
            ins = fn()
        eng.cnt += 1
        ins.then_inc(eng.sem, 1)
        self._record((eng.sem, eng.cnt), r, w)

    def dma(self, eng, out, in_, r=(), w=(), sb=None):
        if sb is None:
            sb = (list(w) + list(r))[0]
        if sb.dsem is None:
            if sb.name not in self.semcache:
                self.semcache[sb.name] = [self.nc.alloc_semaphore("d_" + sb.name), 0]
                self.dsems.append(self.semcache[sb.name])
            sb.dsem = self.semcache[sb.name]
        self._waits(eng, r, w)
        ins = eng.h.dma_start(out=out, in_=in_)
        sb.dsem[1] += 16
        ins.then_inc(sb.dsem[0], 16)
        self._record((sb.dsem[0], sb.dsem[1]), r, w)
        return ins

    def fence(self, waiter, others):
        for o in others:
            if o.cnt > 0:
                self._wait(waiter, o.sem, o.cnt)

    def barrier(self):
        for e in self.engs:
            self.fence(e, self.engs)
            for ds in self.dsems:
                if ds[1] > 0:
                    self._wait(e, ds[0], ds[1])


def build(cfg):
    c = cfg
    D, F, T, L, H, GH, TT, KD, KF, NT = c.D, c.F, c.T, c.L, c.H, c.GH, c.TT, c.KD, c.KF, c.NT
    E, DK, DV, PAD = c.E, c.DK, c.DV, c.PAD
    NQ = c.QK // 128
    nc = bass.Bass("TRN2", target_bir_lowering=False)
    K = KB(nc)
    pe, act, dve, pool, sp = K.pe, K.act, K.dve, K.pool, K.sp
    uid = [0]

    def din(name, shape, dt=F32):
        return nc.dram_tensor(name, list(shape), dt, kind="ExternalInput").ap()

    xT = din("xT", [D, T])
    W = {}
    NGK_, NAK_ = c.VW // 128, c.AKV // 128
    NFM = 2 * c.QK + c.AQ + c.AKV + 2 * D
    NTM = 2 * c.VW + c.AKV
    class WT:
        def __init__(self, nm, MC, KC, cw):
            self.MC, self.KC, self.cw = MC, KC, cw
            self.ap = din(nm, [L * MC * 128, KC * cw])

        def __getitem__(self, l):
            return (self, l)

    for nm, MC, KC, cw in (("f1_wg", KF, KD, 128), ("f1_wu", KF, KD, 128), ("f1_wd", KD, KF, 128),
                           ("f2_wg", KF, KD, 128), ("f2_wu", KF, KD, 128), ("f2_wd", KD, KF, 128),
                           ("w_fm", NFM // 128, KD, 128), ("w_tm", NTM // 256, KD, 256),
                           ("proj_gla", KD, NGK_, 128), ("proj_attn", KD, NAK_, 128), ("w_out", KD, KD, 128)):
        W[nm] = WT(nm, MC, KC, cw)
    w_lr_d = din("w_lr", [L * 128, KD * 32])
    FM_GQ, FM_GK = 0, c.QK // 128
    FM_AQ = 2 * c.QK // 128
    FM_AK = FM_AQ + c.AQ // 128
    FM_G1 = FM_AK + c.AKV // 128
    FM_G2 = FM_G1 + KD
    gains_d = din("gains", [128, 6 * L * KD])
    up_d = din("gla_up", [16, L * 2 * c.QK])
    gbias_d = din("gla_bias", [128, L * 2 * NQ])
    gng_d = din("gla_ng", [128, L * c.VW])
    abias_d = din("abias", [128, c.NG * H * 5 * 256])
    tri_d = din("tri", [128, 2 * 128])
    ident_d = din("ident", [128, 128])
    flag_d = din("flag", [128, 1])
    yT = nc.dram_tensor("yT", [D, T], F32, kind="ExternalOutput").ap()

    def dscr(name, shape, dt):
        if getattr(c, "debug", False):
            return nc.dram_tensor(name, list(shape), dt, kind="ExternalOutput").ap()
        return nc.dram_tensor(name, list(shape), dt).ap()

    xa = dscr("xa", [D, T], F32)
    xb = dscr("xb", [D, T], F32)
    xc = dscr("xc", [D, T], F32)
    yraw = dscr("yraw", [D, TT], F32)
    gqT = dscr("gqT", [c.QK, T], BF16)
    gkT = dscr("gkT", [c.QK, T], BF16)
    gv_t = dscr("gv_t", [T, c.VW], BF16)
    gr_t = dscr("gr_t", [T, c.VW], BF16)
    cs_f = dscr("cs_f", [c.QK, T], F32)
    cs_b = dscr("cs_b", [c.QK, T], F32)
    ce_b = dscr("ce_b", [c.QK, T], F32)
    aqT = dscr("aqT", [c.AQ, T], BF16)
    akT = dscr("akT", [c.AKV, T], BF16)
    av_t = dscr("av_t", [T + 2 * PAD, c.AKV], BF16)
    oaT = dscr("oaT", [c.AKV, T], BF16)
    ogT = dscr("ogT", [c.VW, T], BF16)
    of_s = dscr("of_s", [T, DV], F32)

    def tb(name):
        return [Buf(f"{name}{i}", nowaw=True) for i in range(NT)]

    B_xin, B_xa, B_xb, B_xc, B_yo = tb("xin"), tb("xa"), tb("xb"), tb("xc"), tb("yo")
    B_yraw = Buf("yraw")
    B_proj = Buf("proj", nowaw=True)
    B_mix = Buf("mix", nowaw=True)
    B_of = Buf("of", nowaw=True)
    B_avpad = Buf("avpad", nowaw=True)

    def sb(name, shape, dt):
        return nc.alloc_sbuf_tensor("s_" + name, list(shape), dt)

    gains = sb("gains", [128, 6 * L * KD], F32); B_gains = Buf("gains")
    ghalf = sb("ghalf", [128, 6 * L * KD], F32); B_ghalf = Buf("ghalf")
    ones = sb("ones", [128, 128], BF16); B_ones = Buf("ones")
    ident = sb("ident", [128, 128], BF16); B_ident = Buf("ident")
    tri = sb("tri", [128, 2 * 128], F32); B_tri = Buf("tri")
    flag = sb("flag", [128, 1], F32); B_flag = Buf("flagb")
    up_sb = sb("up_sb", [16, L * 2 * c.QK], BF16); B_up = Buf("up")
    gbias = sb("gbias", [128, L * 2 * NQ], F32); B_gbias = Buf("gbias")
    ngbias = sb("ngbias", [128, L * 2 * NQ], F32); B_ngbias = Buf("ngbias")
    rmask = sb("rmask", [128, TT], F32); B_rmask = Buf("rmask")
    stg = [sb(f"stg{i}", [128, TT], F32) for i in range(3)]; B_stg = [Buf(f"stg{i}") for i in range(3)]
    stq = [sb(f"stq{i}", [128, TT], BF16) for i in range(3)]; B_stq = [Buf(f"stq{i}") for i in range(3)]
    rstd = sb("rstd", [128, TT], F32); B_rstd = Buf("rstd")
    rstd2 = sb("rstd2", [128, TT], F32); B_rstd2 = Buf("rstd2")
    xres = [sb(f"xres{i}", [128, TT], F32) for i in range(2)]; B_xres = [Buf(f"xres{i}") for i in range(2)]
    lrs = [sb(f"lrs{i}", [16, TT], BF16) for i in range(2)]; B_lrs = [Buf(f"lrs{i}") for i in range(2)]

    PS = [nc.alloc_psum_tensor(f"ps{i}", [128, 512], F32) for i in range(7)]
    B_ps = [Buf(f"ps{i}") for i in range(7)]
    PSB = nc.alloc_psum_tensor("psb", [128, 1024], BF16); B_psb = Buf("psb")
    SSB = 6

    K.dma(sp, gains[:], gains_d, w=[B_gains])
    K.dma(sp, tri[:], tri_d, w=[B_tri])
    K.dma(sp, flag[:], flag_d, w=[B_flag])
    K.dma(sp, gbias[:], gbias_d, w=[B_gbias])
    K.dma(pool, ident[:], ident_d, w=[B_ident])
    K.dma(pool, up_sb[:], up_d, w=[B_up])
    K.op(dve, lambda: nc.vector.memset(ones[:], 1.0), w=[B_ones])
    K.op(dve, lambda: nc.vector.memset(rmask[:], 1.0), w=[B_rmask])
    for q in range(TT // 128):
        K.op(dve, lambda q=q: nc.vector.memset(rmask[:, q * 128:q * 128 + 1], 0.0), w=[B_rmask])
    K.op(dve, lambda: nc.vector.tensor_scalar_mul(ghalf[:], gains[:], 0.5), r=[B_gains], w=[B_ghalf])
    K.op(dve, lambda: nc.vector.tensor_scalar_mul(ngbias[:], gbias[:], -1.0), r=[B_gbias], w=[B_ngbias])
    K.op(dve, lambda: nc.vector.memset(stq[0][:], 0.0), w=[B_stq[0]])
    zr = 0
    while zr < PAD:
        n = min(128, PAD - zr)
        for base in (zr, PAD + T + zr):
            ww = min(c.AKV, TT)
            for c0 in range(0, c.AKV, ww):
                K.dma(sp, av_t[base:base + n, c0:c0 + ww], stq[0][0:n, 0:ww], r=[B_stq[0]], w=[B_avpad],
                      sb=B_stq[0])
        zr += n

    def gcol(which, l, ch, half=False):
        t = ghalf if half else gains
        i = (which * L + l) * KD + ch
        return t[:, i:i + 1]

    def ftile(ap, ch, ti):
        return ap[ch * 128:(ch + 1) * 128, ti * TT:(ti + 1) * TT]

    NSLOT = 4
    SLOT_E = 4096
    dense = {}

    class Stream:
        gctr = 0

        def __init__(self, slabs, hold=1):
            self.slabs = slabs
            self.hold = hold
            self.i = 0
            self.base = Stream.gctr
            Stream.gctr += len(slabs)

        def _issue(self):
            ap, kn, ncols = self.slabs[self.i]
            assert kn * ncols <= SLOT_E
            s = (self.base + self.i) % NSLOT
            out = dense["slots"][s][:, 0:kn * ncols]
            K.dma(pool, out, ap, w=[dense["B_slots"][s]])
            self.i += 1

        def get(self, j):
            while self.i < min(j + NSLOT - self.hold + 1, len(self.slabs)):
                self._issue()
            assert self.i > j
            s = (self.base + j) % NSLOT
            return dense["slots"][s], dense["B_slots"][s]

    def tslab(wt_l, m, k0, nk):
        wt, l = wt_l
        cw = wt.cw
        r0 = (l * wt.MC + m) * 128
        return (wt.ap[r0:r0 + 128, k0 * cw:(k0 + nk) * cw], nk, cw)

    def kparts(nk, maxk):
        out = []
        k0 = 0
        while k0 < nk:
            n = min(maxk, nk - k0)
            out.append((k0, n))
            k0 += n
        return out

    def norm_tile(src, Bsrc, ti, which, l, xn, B_xn):
        ss, B_ss = PS[SSB], B_ps[SSB]
        for ch in range(KD):
            i = ch % 3
            K.dma(sp, stg[i][:], ftile(src, ch, ti), r=[Bsrc[ti]], w=[B_stg[i]])
            K.op(act, lambda i=i: nc.scalar.activation(out=stq[i][:], in_=stg[i][:], func=AF.Square),
                 r=[B_stg[i]], w=[B_stq[i]])
            K.op(pe, lambda i=i, ch=ch: nc.tensor.matmul(ss[:, 0:TT], ones[:], stq[i][:], start=(ch == 0),
                                                          stop=(ch == KD - 1)),
                 r=[B_stq[i], B_ones], w=[B_ss])
        K.op(act, lambda: nc.scalar.activation(out=rstd[:], in_=ss[:, 0:TT], func=AF.Sqrt, bias=epsc[:, 0:1],
                                               scale=1.0 / D), r=[B_ss, B_epsc], w=[B_rstd])
        K.op(dve, lambda: nc.vector.reciprocal(rstd[:], rstd[:]), r=[B_rstd], w=[B_rstd])
        for ch in range(KD):
            i = ch % 3
            K.dma(sp, stg[i][:], ftile(src, ch, ti), r=[Bsrc[ti]], w=[B_stg[i]])
            K.op(dve, lambda i=i, ch=ch: nc.vector.scalar_tensor_tensor(
                out=xn[:, ch, :], in0=stg[i][:], scalar=gcol(which, l, ch), in1=rstd[:],
                op0=ALU.mult, op1=ALU.mult), r=[B_stg[i], B_rstd, B_gains], w=[B_xn])

    epsc = sb("epsc", [128, 2], F32); B_epsc = Buf("epsc")
    K.op(dve, lambda: nc.vector.memset(epsc[:, 0:1], EPS), w=[B_epsc])
    K.op(dve, lambda: nc.vector.memset(epsc[:, 1:2], 1.0), w=[B_epsc])

    def residual_tile(xsrc, Bx, dst, Bd, ti, which, l, half):
        ss, B_ss = PS[SSB], B_ps[SSB]
        B_gg = B_ghalf if half else B_gains
        K.op(act, lambda: nc.scalar.activation(out=rstd2[:], in_=ss[:, 0:TT], func=AF.Sqrt, bias=epsc[:, 0:1],
                                               scale=1.0 / D), r=[B_ss, B_epsc], w=[B_rstd2])
        K.op(dve, lambda: nc.vector.reciprocal(rstd2[:], rstd2[:]), r=[B_rstd2], w=[B_rstd2])
        for ch in range(KD):
            i = ch % 3
            K.dma(sp, stg[i][:], yraw[ch * 128:(ch + 1) * 128, :], r=[B_yraw], w=[B_stg[i]])
            K.op(dve, lambda i=i: nc.vector.tensor_tensor(out=stg[i][:], in0=stg[i][:], in1=rstd2[:],
                                                          op=ALU.mult), r=[B_stg[i], B_rstd2], w=[B_stg[i]])
            xs, Bxs = xres[ch % 2], B_xres[ch % 2]
            K.dma(sp, xs[:], ftile(xsrc, ch, ti), r=[Bx[ti]], w=[Bxs])
            K.op(dve, lambda i=i, ch=ch, xs=xs: nc.vector.scalar_tensor_tensor(
                out=xs[:], in0=stg[i][:], scalar=gcol(which, l, ch, half), in1=xs[:],
                op0=ALU.mult, op1=ALU.add), r=[B_stg[i], Bxs, B_gg], w=[Bxs])
            K.dma(sp, ftile(dst, ch, ti), xs[:], r=[Bxs], w=[Bd[ti]], sb=Bxs)

    def yraw_chunk(ps, B_p, m, nchunks):
        i = m % 3
        K.op(act, lambda: nc.scalar.activation(out=stg[i][:], in_=ps[:, 0:TT], func=AF.Copy),
             r=[B_p], w=[B_stg[i]])
        K.op(act, lambda: nc.scalar.activation(out=stq[i][:], in_=ps[:, 0:TT], func=AF.Square),
             r=[B_p], w=[B_stq[i]])
        K.op(pe, lambda: nc.tensor.matmul(PS[SSB][:, 0:TT], ones[:], stq[i][:], start=(m == 0),
                                          stop=(m == nchunks - 1)), r=[B_stq[i], B_ones], w=[B_ps[SSB]])
        K.dma(sp, yraw[m * 128:(m + 1) * 128, :], stg[i][:], r=[B_stg[i]], w=[B_yraw], sb=B_stg[i])

    def mm_fm(p, tw, ncols, q, src, nk, k0=0, ktot=None):
        ktot = nk if ktot is None else ktot
        return [lambda k=k: nc.tensor.matmul(
            p[:, 0:TT], tw[:, k * ncols + q * 128:k * ncols + q * 128 + 128], src[:, k0 + k, :],
            start=(k0 + k == 0), stop=(k0 + k == ktot - 1)) for k in range(nk)]

    def ffn_tile(l, pfx, which_pre, which_post, src, Bsrc, dst, Bdst, ti):
        xn, B_xn, hbuf, B_h = dense["xn"], dense["B_xn"], dense["hbuf"], dense["B_h"]
        norm_tile(src, Bsrc, ti, which_pre, l, xn, B_xn)
        wg, wu, wd = W[pfx + "_wg"][l], W[pfx + "_wu"][l], W[pfx + "_wd"][l]
        slabs = []
        for m in range(KF):
            slabs += [tslab(wg, m, 0, KD), tslab(wu, m, 0, KD)]
        st = Stream(slabs, hold=2)
        K.fence(dve, [pe])
        for m in range(KF):
            tg, Bg = st.get(2 * m)
            tu, Bu = st.get(2 * m + 1)
            pg, pu = PS[m % 2], PS[2 + m % 2]
            Bpg, Bpu = B_ps[m % 2], B_ps[2 + m % 2]
            K.group(pe, mm_fm(pg, tg, 128, 0, xn, KD), r=[Bg, B_xn], w=[Bpg])
            K.group(pe, mm_fm(pu, tu, 128, 0, xn, KD), r=[Bu, B_xn], w=[Bpu])
            i = m % 3
            K.op(act, lambda: nc.scalar.activation(out=stg[i][:], in_=pg[:, 0:TT], func=AF.Silu),
                 r=[Bpg], w=[B_stg[i]])
            K.op(dve, lambda m=m: nc.vector.tensor_tensor(out=hbuf[:, m, :], in0=pu[:, 0:TT],
                                                          in1=stg[i][:], op=ALU.mult),
                 r=[Bpu, B_stg[i]], w=[B_h])
        parts = kparts(KF, 32)
        slabs = []
        for mo in range(KD):
            for (k0, n) in parts:
                slabs.append(tslab(wd, mo, k0, n))
        st = Stream(slabs)
        si = 0
        for mo in range(KD):
            p, Bp = PS[4 + mo % 2], B_ps[4 + mo % 2]
            for (k0, n) in parts:
                tw, Bw = st.get(si)
                si += 1
                K.group(pe, mm_fm(p, tw, 128, 0, hbuf, n, k0, KF), r=[Bw, B_h], w=[Bp])
            yraw_chunk(p, Bp, mo, KD)
        residual_tile(src, Bsrc, dst, Bdst, ti, which_post, l, True)

    def proj_tile(l, ti):
        xn, B_xn = dense["xn"], dense["B_xn"]
        wfm, wtm = W["w_fm"][l], W["w_tm"][l]
        t0 = ti * TT
        slabs, meta = [], []
        for fm0, n, dst in ((FM_GQ, c.QK, gqT), (FM_GK, c.QK, gkT), (FM_AQ, c.AQ, aqT), (FM_AK, c.AKV, akT)):
            for cc in range(0, n, 128):
                slabs.append(tslab(wfm, fm0 + cc // 128, 0, KD))
                meta.append((dst, cc))
        st = Stream(slabs)
        for m in range(len(slabs)):
            tw, Bw = st.get(m)
            dst, row = meta[m]
            p, Bp = PS[m % 4], B_ps[m % 4]
            K.group(pe, mm_fm(p, tw, 128, 0, xn, KD), r=[Bw, B_xn], w=[Bp])
            i = m % 3
            K.op(act, lambda: nc.scalar.activation(out=stq[i][:], in_=p[:, 0:TT], func=AF.Copy),
                 r=[Bp], w=[B_stq[i]])
            K.dma(sp, dst[row:row + 128, t0:t0 + TT], stq[i][:], r=[B_stq[i]], w=[B_proj], sb=B_stq[i])
        st = Stream([(w_lr_d[l * 128:(l + 1) * 128, :], KD, 32)])
        tw, Bw = st.get(0)
        for dr in range(2):
            p, Bp = PS[dr], B_ps[dr]
            K.group(pe, [lambda k=k, dr=dr: nc.tensor.matmul(
                p[0:16, 0:TT], tw[:, k * 32 + dr * 16:k * 32 + dr * 16 + 16], xn[:, k, :],
                start=(k == 0), stop=(k == KD - 1)) for k in range(KD)], r=[Bw, B_xn], w=[Bp])
            K.op(act, lambda dr=dr: nc.scalar.activation(out=lrs[dr][:], in_=p[0:16, 0:TT], func=AF.Copy),
                 r=[Bp], w=[B_lrs[dr]])
        m = 0
        for dr in range(2):
            for q in range(NQ):
                p, Bp = PS[2 + m % 2], B_ps[2 + m % 2]
                ucol = (l * 2 + dr) * c.QK + q * 128
                K.op(pe, lambda: nc.tensor.matmul(p[:, 0:TT], up_sb[:, ucol:ucol + 128], lrs[dr][:],
                                                  start=True, stop=True), r=[B_up, B_lrs[dr]], w=[Bp])
                bi = (l * 2 + dr) * NQ + q
                i = m % 3
                K.op(act, lambda: nc.scalar.activation(out=stg[i][:], in_=p[:, 0:TT], func=AF.Exp,
                                                       bias=ngbias[:, bi:bi + 1], scale=-1.0),
                     r=[Bp, B_ngbias], w=[B_stg[i]])
                K.op(act, lambda: nc.scalar.activation(out=stg[i][:], in_=stg[i][:], func=AF.Ln,
                                                       bias=epsc[:, 1:2], scale=1.0),
                     r=[B_stg[i], B_epsc], w=[B_stg[i]])
                xs, Bx = xres[m % 2], B_xres[m % 2]
                K.op(dve, lambda: nc.vector.tensor_tensor_scan(out=xs[:], data0=rmask[:], data1=stg[i][:],
                                                               initial=0.0, op0=ALU.mult, op1=ALU.add),
                     r=[B_stg[i], B_rmask], w=[Bx])
                dst = cs_f if dr == 0 else cs_b
                K.dma(sp, dst[q * 128:(q + 1) * 128, t0:t0 + TT], xs[:], r=[Bx], w=[B_proj], sb=Bx)
                if dr == 1:
                    K.op(dve, lambda: nc.vector.tensor_tensor(out=stg[i][:], in0=xs[:], in1=stg[i][:],
                                                              op=ALU.subtract), r=[Bx, B_stg[i]], w=[B_stg[i]])
                    K.dma(sp, ce_b[q * 128:(q + 1) * 128, t0:t0 + TT], stg[i][:], r=[B_stg[i]], w=[B_proj],
                          sb=B_stg[i])
                m += 1
        parts = kparts(KD, 16)
        slabs, meta = [], []
        for tm0, n, dst, roff, fn in ((0, c.VW, gv_t, 0, AF.Copy), (c.VW // 256, c.VW, gr_t, 0, AF.Silu),
                                      (2 * c.VW // 256, c.AKV, av_t, PAD, AF.Copy)):
            for cc in range(0, n, 256):
                for (k0, nk) in parts:
                    slabs.append(tslab(wtm, tm0 + cc // 256, k0, nk))
                meta.append((dst, roff, fn, cc))
        st = Stream(slabs)
        si = 0
        m = 0
        NS4 = TT // 128
        for (dst, roff, fn, cc) in meta:
            for (k0, nk) in parts:
                tw, Bw = st.get(si)
                si += 1
                for s4 in range(NS4):
                    p, Bp = PS[s4], B_ps[s4]
                    K.group(pe, [lambda k=k, s4=s4: nc.tensor.matmul(
                        p[:, 0:256], xn[:, k0 + k, s4 * 128:(s4 + 1) * 128], tw[:, k * 256:(k + 1) * 256],
                        start=(k0 + k == 0), stop=(k0 + k == KD - 1)) for k in range(nk)],
                            r=[Bw, B_xn], w=[Bp])
            for s4 in range(NS4):
                p, Bp = PS[s4], B_ps[s4]
                i = m % 3
                m += 1
                K.op(act, lambda: nc.scalar.activation(out=stq[i][:, 0:256], in_=p[:, 0:256], func=fn),
                     r=[Bp], w=[B_stq[i]])
                r0 = roff + t0 + s4 * 128
                K.dma(sp, dst[r0:r0 + 128, cc:cc + 256], stq[i][:, 0:256], r=[B_stq[i]], w=[B_proj],
                      sb=B_stq[i])

    def attention(l):
        with ExitStack() as es:
            def sc_(name, shape, dt):
                uid[0] += 1
                return es.enter_context(nc.sbuf_tensor(f"{name}_{uid[0]}", list(shape), dt))

            kpad = sc_("kpad", [128, T + 2 * PAD], BF16); B_k = Buf("kpad")
            qs = sc_("qs", [128, T], BF16); B_q = Buf("qs")
            nkt_max = T // 128 + 1
            vt = [sc_(f"vt{i}", [128, nkt_max * 128], BF16) for i in range(2)]
            B_vt = [Buf(f"vt{i}") for i in range(2)]
            US = sc_("US", [128, 2 * T], F32); B_US = Buf("US")
            abt = sc_("abt", [128, 3 * 5 * 256], F32); B_abt = Buf("abt")
            tmp = [sc_(f"atmp{i}", [128, 256], F32) for i in range(2)]
            B_tmp = [Buf(f"atmp{i}") for i in range(2)]
            pt = [sc_(f"apt{i}", [128, 256], BF16) for i in range(2)]
            B_pt = [Buf(f"apt{i}") for i in range(2)]
            K.op(dve, lambda: nc.vector.memset(kpad[:, 0:PAD], 0.0), w=[B_k])
            K.op(dve, lambda: nc.vector.memset(kpad[:, PAD + T:PAD + T + PAD], 0.0), w=[B_k])
            scale = float(E) ** -0.5
            vi = 0
            it = 0
            for hd in range(H):
                K.dma(sp, kpad[:, PAD:PAD + T], akT[hd * 128:(hd + 1) * 128, :], r=[B_proj], w=[B_k])
                K.dma(sp, abt[:].rearrange("p (g x) -> p g x", g=3),
                      abias_d.rearrange("p (g h x) -> p g h x", g=3, h=H)[:, :, hd, :], w=[B_abt])
                for g in range(3):
                    r0 = (g * H + hd) * 128
                    K.dma(sp, qs[:], aqT[r0:r0 + 128, :], r=[B_proj], w=[B_q])
                    d = c.DIL[g]
                    Lr = T // d
                    J = Lr // 128
                    nkt = J + 1
                    for r in range(d):
                        v, Bv = vt[vi % 2], B_vt[vi % 2]
                        vi += 1
                        start = PAD - 64 * d + r
                        src = av_t[start:start + (nkt * 128 - 1) * d + 1:d, hd * 128:(hd + 1) * 128]
                        K.dma(sp, v[:, 0:nkt * 128].rearrange("p (k e) -> p k e", e=128),
                              src.rearrange("(k p) e -> p k e", p=128), r=[B_proj, B_avpad], w=[Bv])
                        for j in range(J):
                            if j == 0:
                                var = 1
                            elif j == J - 1:
                                var = 2
                            elif j == J // 2:
                                var = 3
                            elif j == J // 2 - 1:
                                var = 4
                            else:
                                var = 0
                            bt = abt[:, (g * 5 + var) * 256:(g * 5 + var + 1) * 256]
                            pst, Bst = PS[it % 2], B_ps[it % 2]
                            pso, Bso = PS[2 + it % 2], B_ps[2 + it % 2]
                            tm_, Btm = tmp[it % 2], B_tmp[it % 2]
                            p_, Bp_ = pt[it % 2], B_pt[it % 2]
                            it += 1
                            qcol = r + 128 * j * d
                            qap = qs[:, qcol:qcol + 127 * d + 1:d]
                            ka = PAD + (128 * j - 64) * d + r
                            kb_ = ka + 128 * d
                            K.group(pe, [
                                lambda: nc.tensor.matmul(pst[:, 0:128], kpad[:, ka:ka + 127 * d + 1:d], qap,
                                                         start=True, stop=True),
                                lambda: nc.tensor.matmul(pst[:, 128:256], kpad[:, kb_:kb_ + 127 * d + 1:d], qap,
                                                         start=True, stop=True)],
                                    r=[B_k, B_q], w=[Bst])
                            K.op(dve, lambda: nc.vector.scalar_tensor_tensor(
                                out=tm_[:], in0=pst[:, 0:256], scalar=scale, in1=bt, op0=ALU.mult,
                                op1=ALU.add), r=[Bst, B_abt], w=[Btm])
                            K.op(act, lambda: nc.scalar.activation(out=p_[:], in_=tm_[:], func=AF.Exp),
                                 r=[Btm], w=[Bp_])
                            K.group(pe, [
                                lambda: nc.tensor.matmul(pso[:, 0:128], v[:, j * 128:(j + 1) * 128],
                                                         p_[:, 0:128], start=True, stop=False),
                                lambda: nc.tensor.matmul(pso[:, 0:128], v[:, (j + 1) * 128:(j + 2) * 128],
                                                         p_[:, 128:256], start=False, stop=True),
                                lambda: nc.tensor.matmul(pso[:, 128:256], ones[:], p_[:, 0:128],
                                                         start=True, stop=False),
                                lambda: nc.tensor.matmul(pso[:, 128:256], ones[:], p_[:, 128:256],
                                                         start=False, stop=True)],
                                    r=[Bv, Bp_, B_ones], w=[Bso])
                            usl = US[:].rearrange("p (a t) -> p a t", a=2)[:, :, qcol:qcol + 127 * d + 1:d]
                            pin = pso[:, 0:256].rearrange("p (a t) -> p a t", a=2)
                            if g == 0:
                                K.op(dve, lambda: nc.vector.tensor_copy(out=usl, in_=pin), r=[Bso], w=[B_US])
                            else:
                                K.op(dve, lambda: nc.vector.tensor_tensor(out=usl, in0=pin, in1=usl,
                                                                          op=ALU.add),
                                     r=[Bso, B_US], w=[B_US])
                for t0 in range(0, T, 2048):
                    t1 = min(T, t0 + 2048)
                    K.op(dve, lambda: nc.vector.reciprocal(US[:, T + t0:T + t1], US[:, T + t0:T + t1]),
                         r=[B_US], w=[B_US])
                    K.op(dve, lambda: nc.vector.tensor_tensor(out=qs[:, t0:t1], in0=US[:, t0:t1],
                                                              in1=US[:, T + t0:T + t1], op=ALU.mult),
                         r=[B_US, B_q], w=[B_q])
                K.dma(sp, oaT[hd * 128:(hd + 1) * 128, :], qs[:], r=[B_q], w=[B_mix], sb=B_q)
            K.barrier()

    def gla(l):
        NCH = T // 128
        CPB = TT // 128
        NB = T // TT
        qscale = float(DK) ** -0.5
        with ExitStack() as es:
            def sc_(name, shape, dt):
                uid[0] += 1
                return es.enter_context(nc.sbuf_tensor(f"{name}_{uid[0]}", list(shape), dt))

            def pair(name, shape, dt):
                return [sc_(f"{name}{i}", shape, dt) for i in range(2)], [Buf(f"{name}{i}") for i in range(2)]

            qb, B_qb = pair("gqb", [128, 2, TT], BF16)
            kb, B_kb = pair("gkb", [128, 2, TT], BF16)
            cb, B_cb = pair("gcb", [128, 2, TT], F32)
            c2, B_c2 = pair("gc2", [128, 2, TT], F32)
            vb, B_vb = pair("gvb", [128, CPB, DV], BF16)
            rb, B_rb = pair("grb", [128, CPB, DV], BF16)
            ofb, B_ofb = pair("gofb", [128, CPB, DV], F32)
            en = sc_("gen", [128, 2, TT], F32); B_en = Buf("gen")
            ep = sc_("gep", [128, 2, TT], F32); B_ep = Buf("gep")
            Qd, B_Qd = pair("gQd", [128, 2, TT], BF16)
            Kd, B_Kd = pair("gKd", [128, 2, TT], BF16)
            AT = sc_("gAT", [128, 128], BF16); B_AT = Buf("gAT")
            Kt = sc_("gKt", [128, 2, 128], BF16); B_Kt = Buf("gKt")
            S = sc_("gS", [128, 2, DV], F32); B_S = Buf("gS")
            Ssc = sc_("gSsc", [128, 2, DV], F32); B_Ssc = Buf("gSsc")
            Sbf = sc_("gSbf", [128, 2, DV], BF16); B_Sbf = Buf("gSbf")
            scl = sc_("gscl", [128, 2, NCH], F32); B_scl = Buf("gscl")
            scx = sc_("gscx", [128, 2], F32); B_scx = Buf("gscx")
            ost, B_ost = pair("gost", [128, DV], F32)
            og = sc_("gog", [128, DV], F32); B_og = Buf("gog")
            og2 = sc_("gog2", [128, DV], BF16); B_og2 = Buf("gog2")
            ogs, B_ogs = pair("gogs", [128, DV // 128, 128], BF16)
            gn = sc_("ggn", [128, DV], F32); B_gn = Buf("ggn")
            ssq = sc_("gssq", [128, 2], F32); B_ssq = Buf("gssq")
            junk = sc_("gjunk", [128, DV], BF16); B_junk = Buf("gjunk")
            it = 0
            for gh in range(GH):
                K.dma(sp, gn[:], gng_d[:, l * c.VW + gh * DV:l * c.VW + (gh + 1) * DV], w=[B_gn])
                rows = slice(gh * DK, (gh + 1) * DK)
                for dr in range(2):
                    K.op(dve, lambda: nc.vector.memset(S[:], 0.0), w=[B_S])
                    blocks = range(NB) if dr == 0 else range(NB - 1, -1, -1)
                    for bi, blk in enumerate(blocks):
                        pb = bi % 2
                        t0 = blk * TT
                        K.dma(sp, qb[pb][:], gqT[rows, t0:t0 + TT].rearrange("(k p) t -> p k t", p=128),
                              r=[B_proj], w=[B_qb[pb]])
                        K.dma(sp, kb[pb][:], gkT[rows, t0:t0 + TT].rearrange("(k p) t -> p k t", p=128),
                              r=[B_proj], w=[B_kb[pb]])
                        csrc = cs_f if dr == 0 else ce_b
                        K.dma(sp, cb[pb][:], csrc[rows, t0:t0 + TT].rearrange("(k p) t -> p k t", p=128),
                              r=[B_proj], w=[B_cb[pb]])
                        K.dma(sp, vb[pb][:],
                              gv_t[t0:t0 + TT, gh * DV:(gh + 1) * DV].rearrange("(k p) e -> p k e", p=128),
                              r=[B_proj], w=[B_vb[pb]])
                        if dr == 1:
                            K.dma(sp, c2[pb][:], cs_b[rows, t0:t0 + TT].rearrange("(k p) t -> p k t", p=128),
                                  r=[B_proj], w=[B_c2[pb]])
                            K.dma(sp, rb[pb][:],
                                  gr_t[t0:t0 + TT, gh * DV:(gh + 1) * DV].rearrange("(k p) e -> p k e", p=128),
                                  r=[B_proj], w=[B_rb[pb]])
                            K.dma(sp, ofb[pb][:], of_s[t0:t0 + TT, :].rearrange("(k p) e -> p k e", p=128),
                                  r=[B_of], w=[B_ofb[pb]])
                        sq_, sk_ = (-1.0 / TAU, 1.0 / TAU) if dr == 0 else (1.0 / TAU, -1.0 / TAU)
                        K.op(act, lambda: nc.scalar.activation(out=en[:], in_=cb[pb][:], func=AF.Exp, scale=sq_),
                             r=[B_cb[pb]], w=[B_en])
                        K.op(act, lambda: nc.scalar.activation(out=ep[:], in_=cb[pb][:], func=AF.Exp, scale=sk_),
                             r=[B_cb[pb]], w=[B_ep])
                        K.op(dve, lambda: nc.vector.scalar_tensor_tensor(
                            out=Qd[pb][:], in0=qb[pb][:], scalar=qscale, in1=en[:], op0=ALU.mult, op1=ALU.mult),
                            r=[B_qb[pb], B_en], w=[B_Qd[pb]])
                        K.op(dve, lambda: nc.vector.tensor_tensor(out=Kd[pb][:], in0=kb[pb][:], in1=ep[:],
                                                                  op=ALU.mult),
                             r=[B_kb[pb], B_ep], w=[B_Kd[pb]])
                        ctot = cb[pb] if dr == 0 else c2[pb]
                        B_ct = B_cb[pb] if dr == 0 else B_c2[pb]
                        K.op(act, lambda: nc.scalar.activation(
                            out=scl[:, :, blk * CPB:(blk + 1) * CPB], in_=ctot[:, :, 127:TT:128], func=AF.Exp,
                            scale=-1.0 / TAU), r=[B_ct], w=[B_scl])
                        chunks = range(CPB) if dr == 0 else range(CPB - 1, -1, -1)
                        for ch in chunks:
                            n = blk * CPB + ch
                            cols = slice(ch * 128, (ch + 1) * 128)
                            tok0 = t0 + ch * 128
                            K.group(pe, [lambda k=k: nc.tensor.matmul(
                                PS[0][:, 0:128], Kd[pb][:, k, cols], Qd[pb][:, k, cols], start=(k == 0),
                                stop=(k == 1)) for k in range(2)], r=[B_Kd[pb], B_Qd[pb]], w=[B_ps[0]])
                            K.op(dve, lambda: nc.vector.tensor_tensor(
                                out=AT[:], in0=PS[0][:, 0:128], in1=tri[:, dr * 128:(dr + 1) * 128], op=ALU.mult),
                                r=[B_ps[0], B_tri], w=[B_AT])
                            K.group(pe, [lambda k=k: nc.tensor.transpose(
                                PSB[:, k * 128:(k + 1) * 128], Kd[pb][:, k, cols], ident[:]) for k in range(2)],
                                    r=[B_Kd[pb], B_ident], w=[B_psb])
                            K.op(act, lambda: nc.scalar.activation(
                                out=Kt[:].rearrange("p k t -> p (k t)"), in_=PSB[:, 0:256], func=AF.Copy),
                                r=[B_psb], w=[B_Kt])
                            sidx = n - 1 if dr == 0 else n
                            first = (dr == 0 and n == 0)
                            bnd = (dr == 0 and n == NCH // 2) or (dr == 1 and n == NCH // 2 - 1)
                            if first:
                                K.op(dve, lambda: nc.vector.memset(Ssc[:], 0.0), w=[B_Ssc])
                            else:
                                if bnd:
                                    K.op(dve, lambda: nc.vector.tensor_scalar_mul(
                                        scx[:], scl[:, :, sidx], flag[:, 0:1]), r=[B_scl, B_flag], w=[B_scx])
                                for k in range(2):
                                    sap = scx[:, k:k + 1] if bnd else scl[:, k, sidx:sidx + 1]
                                    K.op(dve, lambda k=k, sap=sap: nc.vector.tensor_scalar_mul(
                                        Ssc[:, k, :], S[:, k, :], sap), r=[B_S, B_scl, B_scx], w=[B_Ssc])
                            K.op(act, lambda: nc.scalar.activation(out=Sbf[:], in_=Ssc[:], func=AF.Copy),
                                 r=[B_Ssc], w=[B_Sbf])
                            K.group(pe, [
                                lambda: nc.tensor.matmul(PS[1][:, 0:DV], AT[:], vb[pb][:, ch, :],
                                                         start=True, stop=False),
                                lambda: nc.tensor.matmul(PS[1][:, 0:DV], Qd[pb][:, 0, cols], Sbf[:, 0, :],
                                                         start=False, stop=False),
                                lambda: nc.tensor.matmul(PS[1][:, 0:DV], Qd[pb][:, 1, cols], Sbf[:, 1, :],
                                                         start=False, stop=True)],
                                    r=[B_AT, B_vb[pb], B_Qd[pb], B_Sbf], w=[B_ps[1]])
                            for k in range(2):
                                K.op(pe, lambda k=k: nc.tensor.matmul(PS[2 + k][:, 0:DV], Kt[:, k, :],
                                                                      vb[pb][:, ch, :], start=True, stop=True),
                                     r=[B_Kt, B_vb[pb]], w=[B_ps[2 + k]])
                                K.op(dve, lambda k=k: nc.vector.tensor_tensor(
                                    out=S[:, k, :], in0=PS[2 + k][:, 0:DV], in1=Ssc[:, k, :], op=ALU.add),
                                    r=[B_ps[2 + k], B_Ssc], w=[B_S])
                            oi = it % 2
                            it += 1
                            if dr == 0:
                                K.op(act, lambda: nc.scalar.activation(out=ost[oi][:], in_=PS[1][:, 0:DV],
                                                                       func=AF.Copy),
                                     r=[B_ps[1]], w=[B_ost[oi]])
                                K.dma(sp, of_s[tok0:tok0 + 128, :], ost[oi][:], r=[B_ost[oi]], w=[B_of],
                                      sb=B_ost[oi])
                            else:
                                K.op(dve, lambda: nc.vector.tensor_tensor(
                                    out=ost[oi][:], in0=PS[1][:, 0:DV], in1=ofb[pb][:, ch, :], op=ALU.add),
                                    r=[B_ps[1], B_ofb[pb]], w=[B_ost[oi]])
                                K.op(act, lambda: nc.scalar.activation(out=og[:], in_=ost[oi][:], func=AF.Square),
                                     r=[B_ost[oi]], w=[B_og])
                                K.op(dve, lambda: nc.vector.reduce_sum(out=ssq[:, 0:1], in_=og[:],
                                                                       axis=mybir.AxisListType.X),
                                     r=[B_og], w=[B_ssq])
                                K.op(act, lambda: nc.scalar.activation(
                                    out=ssq[:, 1:2], in_=ssq[:, 0:1], func=AF.Sqrt, bias=epsc[:, 0:1],
                                    scale=1.0 / DV), r=[B_ssq, B_epsc], w=[B_ssq])
                                K.op(dve, lambda: nc.vector.reciprocal(ssq[:, 1:2], ssq[:, 1:2]),
                                     r=[B_ssq], w=[B_ssq])
                                K.op(dve, lambda: nc.vector.scalar_tensor_tensor(
                                    out=og[:], in0=ost[oi][:], scalar=ssq[:, 1:2], in1=gn[:], op0=ALU.mult,
                                    op1=ALU.mult), r=[B_ost[oi], B_ssq, B_gn], w=[B_og])
                                K.op(dve, lambda: nc.vector.tensor_tensor(
                                    out=og2[:], in0=og[:], in1=rb[pb][:, ch, :], op=ALU.mult),
                                    r=[B_og, B_rb[pb]], w=[B_og2])
                                K.group(pe, [lambda q=q: nc.tensor.transpose(
                                    PSB[:, 256 + q * 128:256 + (q + 1) * 128], og2[:, q * 128:(q + 1) * 128],
                                    ident[:]) for q in range(DV // 128)], r=[B_og2, B_ident], w=[B_psb])
                                K.op(act, lambda: nc.scalar.activation(
                                    out=ogs[oi][:].rearrange("p k t -> p (k t)"), in_=PSB[:, 256:256 + DV],
                                    func=AF.Copy), r=[B_psb], w=[B_ogs[oi]])
                                K.dma(sp, ogT[gh * DV:(gh + 1) * DV, tok0:tok0 + 128].rearrange(
                                    "(k p) t -> p k t", p=128), ogs[oi][:], r=[B_ogs[oi]], w=[B_mix],
                                      sb=B_ogs[oi])
            K.barrier()

    def mixout_tile(l, ti, src, Bsrc, dst, Bdst):
        xn, B_xn, hbuf = dense["xn"], dense["B_xn"], dense["hbuf"]
        t0 = ti * TT
        NGK = c.VW // 128
        NAK = c.AKV // 128
        ogt = hbuf[:, 0:NGK, :]
        oat = hbuf[:, NGK:NGK + NAK, :]
        mrg = hbuf[:, NGK + NAK:NGK + NAK + KD, :]
        B_ogt, B_oat, B_mrg = dense["B_ogt"], dense["B_oat"], dense["B_mrg"]
        otmp, B_otmp = dense["otmp"], dense["B_otmp"]
        K.fence(sp, [pe, dve, act])
        K.fence(dve, [pe])
        norm_tile(src, Bsrc, ti, 2, l, xn, B_xn)
        K.dma(sp, ogt, ogT[:, t0:t0 + TT].rearrange("(k p) t -> p k t", p=128), r=[B_mix], w=[B_ogt])
        K.dma(sp, oat, oaT[:, t0:t0 + TT].rearrange("(k p) t -> p k t", p=128), r=[B_mix], w=[B_oat])
        wfm = W["w_fm"][l]
        slabs = []
        for m in range(KD):
            slabs += [tslab(W["proj_gla"][l], m, 0, NGK), tslab(W["proj_attn"][l], m, 0, NAK),
                      tslab(wfm, FM_G1 + m, 0, KD), tslab(wfm, FM_G2 + m, 0, KD)]
        st = Stream(slabs)
        for m in range(KD):
            par = m % 2
            pA, BA = PS[par], B_ps[par]
            pB, BB = PS[2 + par], B_ps[2 + par]
            tw, Bw = st.get(4 * m)
            K.group(pe, mm_fm(pA, tw, 128, 0, ogt, NGK), r=[Bw, B_ogt], w=[BA])
            tw, Bw = st.get(4 * m + 1)
            K.group(pe, mm_fm(pB, tw, 128, 0, oat, NAK), r=[Bw, B_oat], w=[BB])
            tw, Bw = st.get(4 * m + 2)
            K.group(pe, mm_fm(PS[4], tw, 128, 0, xn, KD), r=[Bw, B_xn], w=[B_ps[4]])
            tw, Bw = st.get(4 * m + 3)
            K.group(pe, mm_fm(PS[5], tw, 128, 0, xn, KD), r=[Bw, B_xn], w=[B_ps[5]])
            ta, Bta = otmp[par], B_otmp[par]
            tb_, Btb = otmp[2 + par], B_otmp[2 + par]
            K.op(act, lambda: nc.scalar.activation(out=ta[:], in_=PS[4][:, 0:TT], func=AF.Sigmoid),
                 r=[B_ps[4]], w=[Bta])
            K.op(act, lambda: nc.scalar.activation(out=tb_[:], in_=PS[5][:, 0:TT], func=AF.Sigmoid),
                 r=[B_ps[5]], w=[Btb])
            K.op(dve, lambda: nc.vector.tensor_tensor(out=ta[:], in0=pA[:, 0:TT], in1=ta[:], op=ALU.mult),
                 r=[BA, Bta], w=[Bta])
            K.op(dve, lambda: nc.vector.tensor_tensor(out=tb_[:], in0=pB[:, 0:TT], in1=tb_[:], op=ALU.mult),
                 r=[BB, Btb], w=[Btb])
            K.op(dve, lambda m=m: nc.vector.tensor_tensor(out=mrg[:, m, :], in0=ta[:], in1=tb_[:], op=ALU.add),
                 r=[Bta, Btb], w=[B_mrg])
        st = Stream([tslab(W["w_out"][l], m, 0, KD) for m in range(KD)])
        for m in range(KD):
            tw, Bw = st.get(m)
            p, Bp = PS[m % 2], B_ps[m % 2]
            K.group(pe, mm_fm(p, tw, 128, 0, mrg, KD), r=[Bw, B_mrg], w=[Bp])
            yraw_chunk(p, Bp, m, KD)
        residual_tile(src, Bsrc, dst, Bdst, ti, 3, l, False)

    def dense_scope(es):
        def sc_(name, shape, dt):
            uid[0] += 1
            return es.enter_context(nc.sbuf_tensor(f"{name}_{uid[0]}", list(shape), dt))
        nh = max(KF, c.VW // 128 + c.AKV // 128 + KD)
        dense["xn"] = sc_("xn", [128, KD, TT], BF16); dense["B_xn"] = Buf("xn")
        dense["hbuf"] = sc_("hbuf", [128, nh, TT], BF16); dense["B_h"] = Buf("hbuf")
        dense["slots"] = [sc_(f"wslot{i}", [128, SLOT_E], BF16) for i in range(NSLOT)]
        dense["B_slots"] = [Buf(f"wslot{i}") for i in range(NSLOT)]
        dense["otmp"] = [sc_(f"otmp{i}", [128, TT], F32) for i in range(4)]
        dense["B_otmp"] = [Buf(f"otmp{i}") for i in range(4)]
        dense["B_ogt"], dense["B_oat"], dense["B_mrg"] = Buf("ogt"), Buf("oat"), Buf("mrg")

    xin, Bin = xT, B_xin
    for l in range(L):
        last = (l == L - 1)
        with ExitStack() as es:
            dense_scope(es)
            for ti in range(NT):
                ffn_tile(l, "f1", 0, 1, xin, Bin, xa, B_xa, ti)
                norm_tile(xa, B_xa, ti, 2, l, dense["xn"], dense["B_xn"])
                proj_tile(l, ti)
            K.barrier()
        attention(l)
        gla(l)
        with ExitStack() as es:
            dense_scope(es)
            xo, Bo = (yT, B_yo) if last else (xc, B_xc)
            for ti in range(NT):
                mixout_tile(l, ti, xa, B_xa, xb, B_xb)
                ffn_tile(l, "f2", 4, 5, xb, B_xb, xo, Bo, ti)
            K.barrier()
        xin, Bin = xc, B_xc
    K.barrier()
    return nc


def alibi_tables(cfg, packed):
    c = cfg
    H = c.H
    n = c.NG * H
    out = np.zeros((128, c.NG, H, 5, 256), np.float32)
    ki = np.arange(128)[:, None]
    qi = np.arange(128)[None, :]
    rel_a = ki - 64 - qi
    rel_b = ki + 64 - qi
    NEG = np.float32(-1e30)
    for g in range(c.NG):
        d = c.DIL[g]
        for hd in range(H):
            slope = np.float32(2.0) ** np.float32(-8.0 * (g * H + hd + 1) / n)
            ba = np.where(np.abs(rel_a) <= 64, -slope * np.float32(d) * np.abs(rel_a).astype(np.float32), NEG)
            bb = np.where(np.abs(rel_b) <= 64, -slope * np.float32(d) * np.abs(rel_b).astype(np.float32), NEG)
            ba_edge = np.where(ki < 64, NEG, ba)
            bb_edge = np.where(ki >= 64, NEG, bb)
            combos = [(ba, bb), (ba_edge, bb), (ba, bb_edge),
                      (ba_edge if packed else ba, bb), (ba, bb_edge if packed else bb)]
            for v, (a_, b_) in enumerate(combos):
                out[:, g, hd, v, 0:128] = a_
                out[:, g, hd, v, 128:256] = b_
    return np.ascontiguousarray(out.reshape(128, -1))


def prep_shared(cfg, p):
    c = cfg
    L, KD = c.L, c.KD
    sh = {}

    def tile_w(w, cw=128):
        w = np.asarray(w, dtype=np.float32)
        L_, K_, M_ = w.shape
        t = np.ascontiguousarray(w.reshape(L_, K_ // 128, 128, M_ // cw, cw).transpose(0, 3, 2, 1, 4))
        return t.reshape(L_ * (M_ // cw) * 128, (K_ // 128) * cw)

    for k in ("f1_wg", "f1_wu", "f1_wd", "f2_wg", "f2_wu", "f2_wd", "proj_gla", "proj_attn", "w_out"):
        sh[k] = tile_w(p[k])
    win = np.asarray(p["w_in"], dtype=np.float32)
    fm = np.concatenate([win[:, :, c.o_gq:c.o_gq + 2 * c.QK], win[:, :, c.o_aq:c.o_aq + c.AQ + c.AKV],
                         win[:, :, c.o_gate:c.o_gate + 2 * c.D]], axis=2)
    sh["w_fm"] = tile_w(fm)
    tm = np.concatenate([win[:, :, c.o_gv:c.o_gv + 2 * c.VW], win[:, :, c.o_av:c.o_av + c.AKV]], axis=2)
    sh["w_tm"] = tile_w(tm, 256)
    lr = win[:, :, c.o_lf:c.o_lf + 32]
    sh["w_lr"] = np.ascontiguousarray(lr.reshape(L, KD, 128, 32).transpose(0, 2, 1, 3)).reshape(L * 128, KD * 32)
    gl = [p["f1_pre"], p["f1_post"], p["mix_pre"], p["mix_post"], p["f2_pre"], p["f2_post"]]
    g = np.stack([np.asarray(a, np.float32) for a in gl], 0)
    g = g.reshape(6, L, KD, 128).transpose(3, 0, 1, 2).reshape(128, 6 * L * KD)
    sh["gains"] = np.ascontiguousarray(g)
    up = np.stack([np.asarray(p["up_f"], np.float32), np.asarray(p["up_b"], np.float32)], 1)
    sh["gla_up"] = np.ascontiguousarray(up.transpose(2, 0, 1, 3).reshape(16, L * 2 * c.QK))
    b = np.stack([np.asarray(p["b_f"], np.float32), np.asarray(p["b_b"], np.float32)], 1)
    nq = c.QK // 128
    sh["gla_bias"] = np.ascontiguousarray(b.reshape(L, 2, nq, 128).transpose(3, 0, 1, 2).reshape(128, L * 2 * nq))
    ng = np.asarray(p["gla_g"], np.float32).reshape(1, L * c.VW)
    sh["gla_ng"] = np.ascontiguousarray(np.broadcast_to(ng, (128, L * c.VW)))
    j = np.arange(128)[:, None]
    i = np.arange(128)[None, :]
    sh["tri"] = np.ascontiguousarray(np.concatenate([(j <= i), (j > i)], 1).astype(np.float32))
    sh["ident"] = np.eye(128, dtype=np.float32)
    return sh


def core_inputs(cfg, sh, tabs, x_slot, packed):
    d = dict(sh)
    d["xT"] = np.ascontiguousarray(np.asarray(x_slot, np.float32).T)
    d["abias"] = tabs[bool(packed)]
    d["flag"] = np.full((128, 1), 0.0 if packed else 1.0, np.float32)
    return d


_CACHE = {}


def run_slots(cfg, params, slots, packed_flags, n_cores=8, core_of_slot=None):
    if "nc" not in _CACHE or _CACHE.get("cfg") is not cfg:
        _CACHE["nc"] = build(cfg)
        _CACHE["cfg"] = cfg
    nc = _CACHE["nc"]
    sh = prep_shared(cfg, params)
    tabs = {False: alibi_tables(cfg, False), True: alibi_tables(cfg, True)}
    if core_of_slot is None:
        core_of_slot = list(range(len(slots)))
    zero_x = np.zeros((cfg.T, cfg.D), np.float32)
    in_maps = []
    for core in range(n_cores):
        if core in core_of_slot:
            s = core_of_slot.index(core)
            in_maps.append(core_inputs(cfg, sh, tabs, slots[s], packed_flags[s]))
        else:
            in_maps.append(core_inputs(cfg, sh, tabs, zero_x, False))
    res = run_bass_kernel_spmd(nc, in_maps, core_ids=list(range(n_cores)))
    _CACHE["res"] = res
    return [np.ascontiguousarray(res.results[core_of_slot[s]]["yT"].T) for s in range(len(slots))]


_FULL = Cfg()


def kernel(x_prompt, x_sample, ffn1_pre_g, ffn1_w_gate, ffn1_w_up, ffn1_w_down, ffn1_post_g,
           mix_pre_g, w_in, gla_decay_up_fwd, gla_decay_bias_fwd, gla_decay_up_bwd,
           gla_decay_bias_bwd, gla_norm_g, proj_gla, proj_attn, w_out, mix_post_g,
           ffn2_pre_g, ffn2_w_gate, ffn2_w_up, ffn2_w_down, ffn2_post_g):
    cfg = _FULL
    params = dict(f1_wg=ffn1_w_gate, f1_wu=ffn1_w_up, f1_wd=ffn1_w_down, f2_wg=ffn2_w_gate, f2_wu=ffn2_w_up,
                  f2_wd=ffn2_w_down, w_in=w_in, proj_gla=proj_gla, proj_attn=proj_attn, w_out=w_out,
                  f1_pre=ffn1_pre_g, f1_post=ffn1_post_g, mix_pre=mix_pre_g, mix_post=mix_post_g,
                  f2_pre=ffn2_pre_g, f2_post=ffn2_post_g, up_f=gla_decay_up_fwd, up_b=gla_decay_up_bwd,
                  b_f=gla_decay_bias_fwd, b_b=gla_decay_bias_bwd, gla_g=gla_norm_g)
    x_prompt = np.asarray(x_prompt, np.float32)
    x_sample = np.asarray(x_sample, np.float32)
    assert x_prompt.shape == (2, 4096, cfg.D) and x_sample.shape == (2, 8192, cfg.D)
    slots = [x_sample[0], x_sample[1], np.concatenate([x_prompt[0], x_prompt[1]], 0)]
    outs = run_slots(cfg, params, slots, [False, False, True], core_of_slot=[0, 1, 4])
    y_sample = np.stack([outs[0], outs[1]], 0)
    y_prompt = np.stack([outs[2][:4096], outs[2][4096:]], 0)
    return (y_prompt.astype(np.float32), y_sample.astype(np.float32))
```
